# Optimizing a Trainium2 kernel written in Bass

```python
import math
import jax
import jax.numpy as jnp
from jax import lax
import numpy as np

D_MODEL = 2048
BATCH = 4
SEQ = 4096
DEPTH = 2

GRID_W = 64
CTX_LEN = 256
HEAD_DIM = 128
A_HEADS = D_MODEL // (2 * HEAD_DIM)
A_KV_HEADS = A_HEADS // 4
B_HEADS = D_MODEL // (4 * HEAD_DIM)
C_HEADS = D_MODEL // (4 * HEAD_DIM)
A_Q = A_HEADS * HEAD_DIM
A_KV = A_KV_HEADS * HEAD_DIM
B_W = B_HEADS * HEAD_DIM
C_W = C_HEADS * HEAD_DIM
MIX_W = A_Q + B_W + C_W
Q_BLOCK = 128
ROPE_THETA = 10000.0
NA_WIN_H = 8
NA_WIN_W = 16
CONV_K = 3
DN_CHUNK = 64
EPS = 1e-6
ALPHA = (2 * DEPTH) ** 0.25
OUT_INIT = (8 * DEPTH) ** -0.25
IN_SPLITS = (A_Q, A_KV, A_KV, A_Q, B_W, B_W, B_W, B_W, 3 * C_W, C_W, 4 * C_HEADS)
IN_W = sum(IN_SPLITS)

kernel_name = 'hybrid_grid_flow_block'


def _layernorm(x, g=None, b=None):
    xf = x.astype(jnp.float32)
    xc = xf - jnp.mean(xf, -1, keepdims=True)
    y = xc * lax.rsqrt(jnp.mean(xc * xc, -1, keepdims=True) + EPS)
    if g is not None:
        y = y * g.astype(jnp.float32) + b.astype(jnp.float32)
    return y.astype(x.dtype)


def _rmsnorm(x, g):
    xf = x.astype(jnp.float32)
    y = xf * lax.rsqrt(jnp.mean(xf * xf, -1, keepdims=True) + EPS) * g.astype(jnp.float32)
    return y.astype(x.dtype)


def _l2norm(x):
    return x * lax.rsqrt(jnp.sum(x * x, -1, keepdims=True) + EPS)


def _heads(t, n):
    return t.reshape(t.shape[:-1] + (n, HEAD_DIM))


def _split_in(p):
    return jnp.split(p, np.cumsum(IN_SPLITS)[:-1].tolist(), axis=-1)


def _axial_rope_tables(n_tokens):
    t = jnp.arange(n_tokens, dtype=jnp.int32)
    row = (t // GRID_W).astype(jnp.float32)
    col = (t % GRID_W).astype(jnp.float32)
    half = HEAD_DIM // 2
    inv_freq = ROPE_THETA ** (-jnp.arange(0, half, 2, dtype=jnp.float32) / half)
    ang = jnp.concatenate([row[:, None] * inv_freq, col[:, None] * inv_freq], -1)
    return jnp.cos(ang), jnp.sin(ang)


def _apply_axial_rope(x, cos, sin):
    B, T, H, D = x.shape
    xf = x.astype(jnp.float32).reshape(B, T, H, 2, 2, D // 4)
    x1, x2 = xf[..., 0, :], xf[..., 1, :]
    c = cos.reshape(T, 1, 2, D // 4)
    s = sin.reshape(T, 1, 2, D // 4)
    out = jnp.stack([x1 * c - x2 * s, x2 * c + x1 * s], axis=-2)
    return out.reshape(B, T, H, D).astype(x.dtype)


def _dense_attention(q, k, v):
    B, T, Hq, D = q.shape
    Hkv = k.shape[2]
    qg = q.reshape(B, T, Hkv, Hq // Hkv, D)
    s = jnp.einsum('bqkgd,bskd->bkgqs', qg, k, preferred_element_type=jnp.float32) * D ** -0.5
    p = jax.nn.softmax(s, axis=-1).astype(v.dtype)
    return jnp.einsum('bkgqs,bskd->bqkgd', p, v).reshape(B, T, Hq * D)


def _grid_attention(q, k, v, k_ctx, v_ctx):
    B, S, _, D = q.shape
    G = A_HEADS // A_KV_HEADS
    k_all = jnp.concatenate([k_ctx, k], axis=1)
    v_all = jnp.concatenate([v_ctx, v], axis=1)
    nblk = S // Q_BLOCK
    qb = q.reshape(B, nblk, Q_BLOCK, A_KV_HEADS, G, D).transpose(1, 0, 2, 3, 4, 5)
    scale = D ** -0.5

    def block(qi):
        s = jnp.einsum('bqkgd,bskd->bkgqs', qi, k_all, preferred_element_type=jnp.float32) * scale
        p = jax.nn.softmax(s, axis=-1).astype(v_all.dtype)
        return jnp.einsum('bkgqs,bskd->bqkgd', p, v_all)

    o = lax.map(block, qb)
    return o.transpose(1, 0, 2, 3, 4, 5).reshape(B, S, A_HEADS * D)


def _neighbourhood_attention(q, k, v, k_ctx, v_ctx, rpb, rows):
    B, S, H, D = q.shape
    wh = min(NA_WIN_H, rows)
    qg = q.reshape(B, rows, GRID_W, H, D).transpose(1, 0, 2, 3, 4)
    kg = k.reshape(B, rows, GRID_W, H, D)
    vg = v.reshape(B, rows, GRID_W, H, D)
    r = jnp.arange(rows, dtype=jnp.int32)
    r0 = jnp.clip(r - wh // 2, 0, rows - wh)
    cq = jnp.arange(GRID_W, dtype=jnp.int32)
    c0 = jnp.clip(cq - NA_WIN_W // 2, 0, GRID_W - NA_WIN_W)
    col_idx = c0[:, None] + jnp.arange(NA_WIN_W, dtype=jnp.int32)
    bias_c = col_idx - cq[:, None] + (NA_WIN_W - 1)
    nw = wh * NA_WIN_W
    scale = D ** -0.5

    def row_block(args):
        q_row, r_q, r_start = args
        k_win = lax.dynamic_slice_in_dim(kg, r_start, wh, axis=1)[:, :, col_idx]
        v_win = lax.dynamic_slice_in_dim(vg, r_start, wh, axis=1)[:, :, col_idx]
        bias_r = r_start + jnp.arange(wh, dtype=jnp.int32) - r_q + (NA_WIN_H - 1)
        bias = rpb[:, bias_r[None, :, None], bias_c[:, None, :]]
        s_win = jnp.einsum('bchd,bicjhd->bhcij', q_row, k_win,
                           preferred_element_type=jnp.float32) * scale + bias.astype(jnp.float32)
        s_ctx = jnp.einsum('bchd,bshd->bhcs', q_row, k_ctx, preferred_element_type=jnp.float32) * scale
        s = jnp.concatenate([s_win.reshape(B, H, GRID_W, nw), s_ctx], axis=-1)
        p = jax.nn.softmax(s, axis=-1).astype(v.dtype)
        p_win = p[..., :nw].reshape(B, H, GRID_W, wh, NA_WIN_W)
        return (jnp.einsum('bhcij,bicjhd->bchd', p_win, v_win)
                + jnp.einsum('bhcs,bshd->bchd', p[..., nw:], v_ctx))

    o = lax.map(row_block, (qg, r, r0))
    return o.transpose(1, 0, 2, 3, 4).reshape(B, S, H * D)


def _short_conv(x, w):
    y = lax.conv_general_dilated(x, w[:, None, :].astype(x.dtype), window_strides=(1,),
                                 padding=[(CONV_K // 2, CONV_K // 2)],
                                 dimension_numbers=('NWC', 'WIO', 'NWC'),
                                 feature_group_count=x.shape[-1])
    return jax.nn.silu(y)


def _dn_qkv(qkv, conv_w):
    y = _short_conv(qkv, conv_w).astype(jnp.float32)
    q, k, v = jnp.split(y, 3, axis=-1)
    q = _l2norm(_heads(q, C_HEADS)) * HEAD_DIM ** -0.5
    k = _l2norm(_heads(k, C_HEADS))
    return q, k, _heads(v, C_HEADS)


def _dn_gates(ab, a_log, dt_bias):
    abf = ab.astype(jnp.float32).reshape(ab.shape[:-1] + (2, 2, C_HEADS))
    beta = jax.nn.sigmoid(abf[..., 0, :])
    g = -jnp.exp(a_log.astype(jnp.float32)) * jax.nn.softplus(abf[..., 1, :] + dt_bias.astype(jnp.float32))
    return beta, g


def _gdn_chunked(q, k, v, g, beta, s0):
    B, T, H, Dk = q.shape
    C = DN_CHUNK
    N = T // C

    def chunks(t):
        return t.reshape(B, N, C, H, t.shape[-1]).transpose(1, 0, 3, 2, 4)

    qc, kc, vc = chunks(q), chunks(k), chunks(v)
    gc = g.reshape(B, N, C, H).transpose(1, 0, 3, 2)
    bc = beta.reshape(B, N, C, H).transpose(1, 0, 3, 2)
    gcum = jnp.cumsum(gc, axis=-1)
    lower = jnp.tril(jnp.ones((C, C), bool))
    strict = jnp.tril(jnp.ones((C, C), bool), -1)
    diff = gcum[..., :, None] - gcum[..., None, :]
    decay = jnp.where(lower, jnp.exp(jnp.where(lower, diff, 0.0)), 0.0)
    kb = kc * bc[..., None]
    lmat = jnp.where(strict, jnp.einsum('nbhid,nbhjd->nbhij', kb, kc) * decay, 0.0)
    a = lmat + jnp.eye(C, dtype=jnp.float32)
    u = lax.linalg.triangular_solve(a, vc * bc[..., None], left_side=True, lower=True, unit_diagonal=True)
    w = lax.linalg.triangular_solve(a, kb * jnp.exp(gcum)[..., None], left_side=True, lower=True,
                                    unit_diagonal=True)
    qk = jnp.einsum('nbhid,nbhjd->nbhij', qc, kc) * decay

    def step(s, xs):
        q_i, k_i, u_i, w_i, g_i, qk_i = xs
        v_new = u_i - jnp.einsum('bhcd,bhde->bhce', w_i, s)
        o = (jnp.einsum('bhcd,bhde->bhce', q_i * jnp.exp(g_i)[..., None], s)
             + jnp.einsum('bhij,bhje->bhie', qk_i, v_new))
        g_last = g_i[..., -1:]
        s = s * jnp.exp(g_last)[..., None] + jnp.einsum(
            'bhcd,bhce->bhde', k_i * jnp.exp(g_last - g_i)[..., None], v_new)
        return s, o

    s_fin, o = lax.scan(step, s0, (qc, kc, u, w, gcum, qk))
    return o.transpose(1, 0, 3, 2, 4).reshape(B, T, H, v.shape[-1]), s_fin


def _bidir_gated_deltanet(qkv, qkv_x, ab, ab_x, conv_w, a_log, dt_bias, with_ctx_out):
    q, k, v = _dn_qkv(qkv, conv_w)
    q_x, k_x, v_x = _dn_qkv(qkv_x, conv_w)
    beta, g = _dn_gates(ab, a_log, dt_bias)
    beta_x, g_x = _dn_gates(ab_x, a_log, dt_bias)
    s0 = jnp.zeros((q.shape[0], C_HEADS, HEAD_DIM, HEAD_DIM), jnp.float32)

    def rev(t):
        return t[:, ::-1]

    oxf, sxf = _gdn_chunked(q_x, k_x, v_x, g_x[:, :, 0], beta_x[:, :, 0], s0)
    of, _ = _gdn_chunked(q, k, v, g[:, :, 0], beta[:, :, 0], sxf)
    oxb, sxb = _gdn_chunked(rev(q_x), rev(k_x), rev(v_x), rev(g_x[:, :, 1]), rev(beta_x[:, :, 1]), s0)
    ob, _ = _gdn_chunked(rev(q), rev(k), rev(v), rev(g[:, :, 1]), rev(beta[:, :, 1]), sxb)
    o_lat = of + rev(ob)
    o_ctx = oxf + rev(oxb) if with_ctx_out else None
    return o_lat, o_ctx


def _layer(x, ctx, mod_lat, mod_ctx, w_in, q_norm, k_norm, rpb, conv_w, a_log, dt_bias, o_norm,
           w_out, ln_g, ln_b, rope_cos, rope_sin, rows, with_ctx_out):
    B, S, _ = x.shape
    L = ctx.shape[1]
    shift, scale, gate = jnp.split(mod_lat, 3, axis=-1)
    shift_x, scale_x, gate_x = jnp.split(mod_ctx, 3, axis=-1)
    h = _layernorm(x) * (1 + scale[:, None, :]) + shift[:, None, :]
    hx = _layernorm(ctx) * (1 + scale_x) + shift_x
    qa, ka, va, za, qb, kb, vb, zb, qkv_c, zc, ab = _split_in(h @ w_in)
    qa_x, ka_x, va_x, za_x, qb_x, kb_x, vb_x, zb_x, qkv_cx, zc_x, ab_x = _split_in(hx @ w_in)

    qa = _apply_axial_rope(_rmsnorm(_heads(qa, A_HEADS), q_norm), rope_cos, rope_sin)
    ka = _apply_axial_rope(_rmsnorm(_heads(ka, A_KV_HEADS), k_norm), rope_cos, rope_sin)
    ka_x = _rmsnorm(_heads(ka_x, A_KV_HEADS), k_norm)
    va_x = _heads(va_x, A_KV_HEADS)
    ya = _grid_attention(qa, ka, _heads(va, A_KV_HEADS), ka_x, va_x) * jax.nn.silu(za)

    kb_x, vb_x = _heads(kb_x, B_HEADS), _heads(vb_x, B_HEADS)
    yb = _neighbourhood_attention(_heads(qb, B_HEADS), _heads(kb, B_HEADS), _heads(vb, B_HEADS),
                                  kb_x, vb_x, rpb, rows) * jax.nn.silu(zb)

    oc, oc_x = _bidir_gated_deltanet(qkv_c, qkv_cx, ab, ab_x, conv_w, a_log, dt_bias, with_ctx_out)
    yc = _rmsnorm(oc, o_norm).astype(h.dtype).reshape(B, S, C_W) * jax.nn.silu(zc)

    y = jnp.concatenate([ya, yb, yc], axis=-1) @ w_out
    x_new = _layernorm(ALPHA * x + gate[:, None, :] * y, ln_g, ln_b)
    if not with_ctx_out:
        return x_new, ctx

    qa_x = _rmsnorm(_heads(qa_x, A_HEADS), q_norm)
    ya_x = _dense_attention(qa_x, ka_x, va_x) * jax.nn.silu(za_x)
    yb_x = _dense_attention(_heads(qb_x, B_HEADS), kb_x, vb_x) * jax.nn.silu(zb_x)
    yc_x = _rmsnorm(oc_x, o_norm).astype(hx.dtype).reshape(B, L, C_W) * jax.nn.silu(zc_x)
    y_x = jnp.concatenate([ya_x, yb_x, yc_x], axis=-1) @ w_out
    ctx_new = _layernorm(ALPHA * ctx + gate_x * y_x, ln_g, ln_b)
    return x_new, ctx_new


def setup_inputs(seed: int = 0) -> dict:
    key = jax.random.key(seed)
    ks = jax.random.split(key, 17)
    nrm = jax.random.normal
    x = nrm(ks[0], (BATCH, SEQ, D_MODEL), jnp.float32)
    c = nrm(ks[1], (BATCH, D_MODEL), jnp.float32)
    ctx = nrm(ks[2], (BATCH, CTX_LEN, D_MODEL), jnp.float32)
    c_ctx = nrm(ks[3], (D_MODEL,), jnp.float32)
    w_mod = nrm(ks[4], (DEPTH, D_MODEL, 3 * D_MODEL), jnp.float32) * D_MODEL ** -0.5
    b_mod = 0.02 * nrm(ks[5], (DEPTH, 3 * D_MODEL), jnp.float32)
    w_in = nrm(ks[6], (DEPTH, D_MODEL, IN_W), jnp.float32) * D_MODEL ** -0.5
    q_norm = 1.0 + 0.02 * nrm(ks[7], (DEPTH, HEAD_DIM), jnp.float32)
    k_norm = 1.0 + 0.02 * nrm(ks[8], (DEPTH, HEAD_DIM), jnp.float32)
    rpb = 0.02 * nrm(ks[9], (DEPTH, B_HEADS, 2 * NA_WIN_H - 1, 2 * NA_WIN_W - 1), jnp.float32)
    conv_w = nrm(ks[10], (DEPTH, CONV_K, 3 * C_W), jnp.float32) * CONV_K ** -0.5
    a_log = jnp.log(jax.random.uniform(ks[11], (DEPTH, 2, C_HEADS), jnp.float32, 1.0, 16.0))
    u = jax.random.uniform(ks[12], (DEPTH, 2, C_HEADS), jnp.float32)
    dt = jnp.exp(u * (math.log(0.1) - math.log(0.001)) + math.log(0.001))
    dt_bias = dt + jnp.log(-jnp.expm1(-dt))
    o_norm = 1.0 + 0.02 * nrm(ks[13], (DEPTH, HEAD_DIM), jnp.float32)
    w_out = nrm(ks[14], (DEPTH, MIX_W, D_MODEL), jnp.float32) * (MIX_W ** -0.5 * OUT_INIT)
    ln_g = 1.0 + 0.02 * nrm(ks[15], (DEPTH, D_MODEL), jnp.float32)
    ln_b = 0.02 * nrm(ks[16], (DEPTH, D_MODEL), jnp.float32)
    return {'x': x, 'c': c, 'ctx': ctx, 'c_ctx': c_ctx, 'w_mod': w_mod, 'b_mod': b_mod,
            'w_in': w_in, 'q_norm': q_norm, 'k_norm': k_norm, 'rpb': rpb, 'conv_w': conv_w,
            'a_log': a_log, 'dt_bias': dt_bias, 'o_norm': o_norm, 'w_out': w_out,
            'ln_g': ln_g, 'ln_b': ln_b}


def reference(x, c, ctx, c_ctx, w_mod, b_mod, w_in, q_norm, k_norm, rpb, conv_w, a_log, dt_bias,
              o_norm, w_out, ln_g, ln_b):
    n_lat = x.shape[1]
    rows = n_lat // GRID_W
    rope_cos, rope_sin = _axial_rope_tables(n_lat)
    for l in range(DEPTH):
        mod_lat = jax.nn.silu(c) @ w_mod[l] + b_mod[l]
        mod_ctx = jax.nn.silu(c_ctx) @ w_mod[l] + b_mod[l]
        x, ctx = _layer(x, ctx, mod_lat, mod_ctx, w_in[l], q_norm[l], k_norm[l], rpb[l], conv_w[l],
                        a_log[l], dt_bias[l], o_norm[l], w_out[l], ln_g[l], ln_b[l],
                        rope_cos, rope_sin, rows, l < DEPTH - 1)
    return x
```

```python
import time
import os
import numpy as np
import ml_dtypes


from contextlib import ExitStack
import concourse.bass as bass
import concourse.mybir as mybir
from concourse.bass_utils import run_bass_kernel_spmd
from concourse.alu_op_type import AluOpType as ALU

AF = mybir.ActivationFunctionType
AX = mybir.AxisListType
F32 = mybir.dt.float32
BF16 = mybir.dt.bfloat16
N_DMA_SEMS = 40


class Tile:
    def __init__(self, name, t):
        self.name = name
        self.t = t
        self.last_w = None
        self.readers = []

    def __getitem__(self, idx):
        return self.t[idx]


class KB:
    ENG = ("pe", "dve", "act", "pool", "sp")

    def __init__(self):
        self.nc = bass.Bass("TRN2", target_bir_lowering=False)
        self.stack = ExitStack()
        nc = self.nc
        self.h = {"pe": nc.tensor, "dve": nc.vector, "act": nc.scalar, "pool": nc.gpsimd, "sp": nc.sync}
        self.sems = {}
        self.count = {}
        for e in self.ENG:
            self.sems[e] = self.stack.enter_context(nc.semaphore("s_" + e))
            self.count[e] = 0
        self.dsems = []
        for i in range(N_DMA_SEMS):
            k = "d%d" % i
            self.sems[k] = self.stack.enter_context(nc.semaphore("s_" + k))
            self.count[k] = 0
            self.dsems.append(k)
        self.dnext = 0
        self.waited = {e: {} for e in self.ENG}
        self.prog = {e: [] for e in self.ENG}
        self.ninst = 0

    def sb(self, name, shape, dtype):
        return Tile(name, self.stack.enter_context(self.nc.sbuf_tensor("S_" + name, list(shape), dtype)))

    def ps(self, name, shape, dtype=F32):
        return Tile(name, self.stack.enter_context(self.nc.psum_tensor("P_" + name, list(shape), dtype)))

    def dram(self, name, shape, dtype, kind):
        return self.nc.dram_tensor(name, list(shape), dtype, kind=kind).ap()

    def _need(self, eng, tok, waits):
        if tok is None:
            return
        k, v = tok
        if eng == "pe" and k == "pe":
            return
        if self.waited[eng].get(k, 0) >= v:
            return
        if waits.get(k, 0) < v:
            waits[k] = v

    def _deps(self, eng, reads, writes):
        waits = {}
        for t in reads:
            self._need(eng, t.last_w, waits)
        for t in writes:
            self._need(eng, t.last_w, waits)
            for r in t.readers:
                self._need(eng, r, waits)
        for k, v in waits.items():
            self.waited[eng][k] = v
        return list(waits.items())

    def _mark(self, tok, reads, writes):
        for t in reads:
            t.readers.append(tok)
            if len(t.readers) > 64:
                m = {}
                for k, v in t.readers:
                    if m.get(k, 0) < v:
                        m[k] = v
                t.readers = list(m.items())
        for t in writes:
            t.last_w = tok
            t.readers = []

    def op(self, eng, fn, reads=(), writes=()):
        waits = self._deps(eng, reads, writes)
        self.count[eng] += 1
        tok = (eng, self.count[eng])
        self._mark(tok, reads, writes)
        self.prog[eng].append((waits, fn, eng, 1))
        self.ninst += 1

    def dma(self, eng, fn, reads=(), writes=()):
        k = self.dsems[self.dnext]
        self.dnext = (self.dnext + 1) % len(self.dsems)
        waits = dict(self._deps(eng, reads, writes))
        if self.count[k] > 0 and self.waited[eng].get(k, 0) < self.count[k]:
            waits[k] = max(waits.get(k, 0), self.count[k])
            self.waited[eng][k] = self.count[k]
        self.count[k] += 16
        tok = (k, self.count[k])
        self._mark(tok, reads, writes)
        self.prog[eng].append((list(waits.items()), fn, k, 16))
        self.ninst += 1

    def finish(self, final_tiles=()):
        fin = [(k, self.count[k]) for k in self.dsems if self.count[k] > 0]
        with self.nc.Block() as block:
            def mk(e):
                def body(h):
                    for waits, fn, sk, inc in self.prog[e]:
                        for k, v in waits:
                            h.wait_ge(self.sems[k], v)
                        fn(h).then_inc(self.sems[sk], inc)
                    if e == "sp":
                        for k, v in fin:
                            h.wait_ge(self.sems[k], v)
                return body
            block.tensor(mk("pe"))
            block.vector(mk("dve"))
            block.scalar(mk("act"))
            block.gpsimd(mk("pool"))
            block.sync(mk("sp"))
        self.stack.close()
        return self.nc


BF = ml_dtypes.bfloat16
GRID_W = 64
ROPE_THETA = 10000.0


def rope_tables_T():
    t = np.arange(4096, dtype=np.int32)
    row = (t // GRID_W).astype(np.float32)
    col = (t % GRID_W).astype(np.float32)
    half = 64
    inv_freq = (np.float32(ROPE_THETA) ** (-np.arange(0, half, 2, dtype=np.float32) / np.float32(half))).astype(np.float32)
    cosT = np.zeros((128, 4096), np.float32)
    sinT = np.zeros((128, 4096), np.float32)
    for d in range(128):
        a = d // 64; p = (d % 64) // 32; f = d % 32
        pos = row if a == 0 else col
        ang = (pos * inv_freq[f]).astype(np.float32)
        cosT[d] = np.cos(ang)
        sinT[d] = np.sin(ang) * (-1.0 if p == 0 else 1.0)
    return cosT, sinT


def consts():
    ident = np.eye(128, dtype=np.float32)
    perm = np.zeros((128, 128), np.float32)
    for dp in range(128):
        p = (dp % 64) // 32
        partner = dp + 32 if p == 0 else dp - 32
        perm[partner, dp] = 1.0
    ones = np.ones((128, 128), np.float32)
    return np.ascontiguousarray(np.stack([ident, perm, ones], axis=1))


def s1_inputs(xcur, ctxcur, mod_l, w_in_l, q_norm_l, k_norm_l):
    cosT, sinT = rope_tables_T()
    cst = consts()
    maps = []
    for core in range(8):
        b = core // 2; hf = core % 2
        x_in = np.concatenate([xcur[b, hf * 2048:(hf + 1) * 2048], ctxcur[b, hf * 128:(hf + 1) * 128]], axis=0)
        shift, scale, gate = np.split(mod_l[b], 3)
        shx, scx, gx = np.split(mod_l[4], 3)
        def fm(v):
            return v.reshape(16, 128).T
        modv = np.ascontiguousarray(np.stack([fm(scale), fm(shift), fm(scx), fm(shx)], axis=1)).astype(np.float32)
        nrm = np.ascontiguousarray(np.stack([q_norm_l, k_norm_l], axis=1)).astype(np.float32)
        rc = np.concatenate([cosT[:, hf * 2048:(hf + 1) * 2048], np.ones((128, 128), np.float32)], axis=1)
        rs = np.concatenate([sinT[:, hf * 2048:(hf + 1) * 2048], np.zeros((128, 128), np.float32)], axis=1)
        maps.append({"x_in": np.ascontiguousarray(x_in), "w_in": w_in_l, "modv": modv, "nrm": nrm,
                     "ropec": np.ascontiguousarray(rc), "ropes": np.ascontiguousarray(rs), "cst": cst})
    return maps


def mod_inputs(c, c_ctx, w_mod, b_mod):
    call = np.zeros((8, 2048), np.float32)
    call[0:4] = c; call[4] = c_ctx
    cT = np.ascontiguousarray(call.T.reshape(16, 128, 8).transpose(1, 0, 2))
    maps = []
    for core in range(8):
        wm = np.zeros((12, 2048, 128), np.float32); bm = np.zeros((128, 12), np.float32)
        for u in range(12):
            gu = core * 12 + u
            l = gu // 48; f = gu % 48
            wm[u] = w_mod[l][:, f * 128:(f + 1) * 128]
            bm[:, u] = b_mod[l][f * 128:(f + 1) * 128]
        maps.append({"cT": cT, "wm": wm, "bm": bm})
    return maps


def mod_assemble(results):
    mod = np.zeros((2, 8, 6144), np.float32)
    for core in range(8):
        o = results[core]["out"]
        for u in range(12):
            gu = core * 12 + u
            l = gu // 48; f = gu % 48
            mod[l][:, f * 128:(f + 1) * 128] = o[u].T
    return mod[:, 0:5]


def nbr_tables(rpb_l):
    NEG = np.float32(-30000.0)
    mb = np.full((4, 128, 5, 5, 128), NEG, np.float32)
    cases = {0: (0, 0), 1: (2, 0), 2: (8, 4), 3: (60, 54), 4: (62, 54)}
    for case, (r, base) in cases.items():
        for dq in range(2):
            rq = r + dq
            r0 = min(max(rq - 4, 0), 56)
            for cq in range(64):
                c0 = min(max(cq - 8, 0), 48)
                ql = dq * 64 + cq
                for rk in range(r0, r0 + 8):
                    kr = rk - base
                    assert 0 <= kr < 10
                    blk = kr // 2
                    for ck in range(c0, c0 + 16):
                        kl = (kr % 2) * 64 + ck
                        mb[:, kl, case, blk, ql] = rpb_l[:, rk - rq + 7, ck - cq + 15]
    return mb


def c_consts():
    j = np.arange(128)[:, None]; i = np.arange(128)[None, :]
    NEG = np.float32(-30000.0)
    ident = np.eye(128, dtype=np.float32)
    ones = np.ones((128, 128), np.float32)
    triF = (j <= i).astype(np.float32)
    triB = (j >= i).astype(np.float32)
    mFi = np.where(i >= j, 0, NEG).astype(np.float32)
    mFs = np.where(i > j, 0, NEG).astype(np.float32)
    mBi = np.where(i <= j, 0, NEG).astype(np.float32)
    mBs = np.where(i < j, 0, NEG).astype(np.float32)
    lv = []
    for k in range(7):
        sz = 1 << k
        lv.append(((j // (2 * sz) == i // (2 * sz)) & (j // sz != i // sz)).astype(np.float32))
    return np.ascontiguousarray(np.stack([ident, ones, triF, triB, mFi, mFs, mBi, mBs] + lv, axis=1))


def s2c_inputs(cC, abT, conv_w_l, a_log_l, dt_bias_l, o_norm_l, heads):
    cst = c_consts()
    maps = []
    for (b, hg) in heads:
        perm = [hg] + [h for h in range(4) if h != hg]
        qkv = np.stack([cC[b][k * 4 + hg] for k in range(3)], 0)
        ab = abT[b].T.reshape(-1, 2, 2, 4)[:, :, :, perm].reshape(34, 128, 16).transpose(1, 0, 2).reshape(128, 34 * 16)
        cw = np.stack([conv_w_l[:, k * 512 + hg * 128:k * 512 + (hg + 1) * 128].T for k in range(3)], axis=1)
        al = np.broadcast_to(a_log_l[:, None, perm], (2, 2, 4)).reshape(16)
        db = np.broadcast_to(dt_bias_l[:, None, perm], (2, 2, 4)).reshape(16)
        gp = np.stack([np.tile(al, 34), np.tile(db, 34)], 0)[None].repeat(128, 0)
        on = np.broadcast_to(o_norm_l[None, :], (128, 128))
        maps.append({"c_qkv": np.ascontiguousarray(qkv, dtype=np.float32), "c_ab": np.ascontiguousarray(ab, dtype=np.float32),
                     "c_cw": np.ascontiguousarray(cw, dtype=np.float32), "c_gp": np.ascontiguousarray(gp, dtype=np.float32),
                     "c_on": np.ascontiguousarray(on, dtype=np.float32), "c_cst": cst})
    return maps


class _NS:
    pass
host = _NS()
for _n in ['rope_tables_T','consts','s1_inputs','mod_inputs','mod_assemble','nbr_tables','c_consts','s2c_inputs']:
    setattr(host, _n, globals()[_n])


EPS = 1e-6
D = 2048
IN_W = 6672
NT1 = 17
NTOK1 = NT1 * 128
COLS = [("qA", 0, 1024, "FM"), ("kA", 1024, 256, "FM"), ("vA", 1280, 256, "TM"), ("zA", 1536, 1024, "TM"),
        ("qB", 2560, 512, "FM"), ("kB", 3072, 512, "FM"), ("vB", 3584, 512, "TM"), ("zB", 4096, 512, "TM"),
        ("cC", 4608, 1536, "FM"), ("zC", 6144, 512, "TM"), ("ab", 6656, 16, "FM")]


def build_mod():
    kb = KB()
    cT = kb.dram("cT", [128, 16, 8], F32, "ExternalInput")
    wm = kb.dram("wm", [12, 2048, 128], F32, "ExternalInput")
    bm = kb.dram("bm", [128, 12], F32, "ExternalInput")
    out = kb.dram("out", [12, 128, 8], F32, "ExternalOutput")
    c_sb = kb.sb("c_sb", [128, 16, 8], F32)
    s_sb = kb.sb("s_sb", [128, 16, 8], F32)
    b_sb = kb.sb("b_sb", [128, 12], F32)
    w_sb = [kb.sb("w_sb%d" % i, [128, 16, 128], F32) for i in range(2)]
    o_sb = [kb.sb("o_sb%d" % i, [128, 8], F32) for i in range(2)]
    pm = [kb.ps("pm%d" % i, [128, 8], F32) for i in range(2)]
    kb.dma("sp", lambda h: h.dma_start(out=c_sb[:], in_=cT), writes=[c_sb])
    kb.dma("sp", lambda h: h.dma_start(out=b_sb[:], in_=bm), writes=[b_sb])
    kb.op("act", lambda h: h.activation(out=s_sb[:], in_=c_sb[:], func=AF.Silu), reads=[c_sb], writes=[s_sb])
    for u in range(12):
        w = w_sb[u % 2]; p = pm[u % 2]; o = o_sb[u % 2]
        kb.dma("sp", lambda h, w=w, u=u: h.dma_start(out=w[:], in_=wm[u].rearrange("(c p) n -> p c n", p=128)), writes=[w])
        for c in range(16):
            kb.op("pe", lambda h, w=w, p=p, c=c: h.matmul(p[:], lhsT=w[:, c, :], rhs=s_sb[:, c, :], start=(c == 0), stop=(c == 15)),
                  reads=[w, s_sb], writes=[p])
        kb.op("dve", lambda h, p=p, o=o, u=u: h.tensor_scalar(out=o[:], in0=p[:], scalar1=b_sb[:, u:u + 1], scalar2=None, op0=ALU.add),
              reads=[p, b_sb], writes=[o])
        kb.dma("sp", lambda h, o=o, u=u: h.dma_start(out=out[u], in_=o[:]), reads=[o])
    return kb.finish()


def s1_items():
    groups = []
    for g in range(14):
        g0 = g * 512
        g1 = min(IN_W, g0 + 512)
        items = []
        for (name, st, wd, kind) in COLS:
            a = max(st, g0); b = min(st + wd, g1)
            if a >= b:
                continue
            if kind == "FM":
                for c0 in range(a, b, 128):
                    items.append((name, "FM", c0, min(128, b - c0), (c0 - st) // 128))
            else:
                items.append((name, "TM", a, b - a, a - st))
        groups.append((g0, g1 - g0, items))
    return groups


def build_s1():
    kb = KB()
    x_in = kb.dram("x_in", [NTOK1, D], F32, "ExternalInput")
    w_in = kb.dram("w_in", [D, IN_W], F32, "ExternalInput")
    modv = kb.dram("modv", [128, 4, 16], F32, "ExternalInput")
    nrm = kb.dram("nrm", [128, 2], F32, "ExternalInput")
    ropec = kb.dram("ropec", [128, NTOK1], F32, "ExternalInput")
    ropes = kb.dram("ropes", [128, NTOK1], F32, "ExternalInput")
    cst = kb.dram("cst", [128, 3, 128], F32, "ExternalInput")
    o_qA = kb.dram("o_qA", [8, 128, NTOK1], BF16, "ExternalOutput")
    o_kA = kb.dram("o_kA", [2, 128, NTOK1], BF16, "ExternalOutput")
    o_qB = kb.dram("o_qB", [4, 128, NTOK1], BF16, "ExternalOutput")
    o_kB = kb.dram("o_kB", [4, 128, NTOK1], BF16, "ExternalOutput")
    o_cC = kb.dram("o_cC", [12, 128, NTOK1], F32, "ExternalOutput")
    o_ab = kb.dram("o_ab", [16, NTOK1], F32, "ExternalOutput")
    o_vA = kb.dram("o_vA", [NTOK1, 256], BF16, "ExternalOutput")
    o_vB = kb.dram("o_vB", [NTOK1, 512], BF16, "ExternalOutput")
    o_sz = kb.dram("o_sz", [NTOK1, 2048], BF16, "ExternalOutput")
    fm_out = {"qA": o_qA, "kA": o_kA, "qB": o_qB, "kB": o_kB, "cC": o_cC}
    tm_out = {"vA": (o_vA, 0), "vB": (o_vB, 0), "zA": (o_sz, 0), "zB": (o_sz, 1024), "zC": (o_sz, 1536)}

    cst_f = kb.sb("cst_f", [128, 3, 128], F32)
    cst_b = kb.sb("cst_b", [128, 3, 128], BF16)
    mod_sb = kb.sb("mod_sb", [128, 4, 16], F32)
    nrm_sb = kb.sb("nrm_sb", [128, 2], F32)
    gq = kb.sb("gq", [128, 1], F32)
    cos_sb = kb.sb("cos_sb", [128, NTOK1], F32)
    sin_sb = kb.sb("sin_sb", [128, NTOK1], F32)
    kb.dma("sp", lambda h: h.dma_start(out=cst_f[:], in_=cst), writes=[cst_f])
    kb.dma("sp", lambda h: h.dma_start(out=mod_sb[:], in_=modv), writes=[mod_sb])
    kb.dma("sp", lambda h: h.dma_start(out=nrm_sb[:], in_=nrm), writes=[nrm_sb])
    kb.dma("sp", lambda h: h.dma_start(out=cos_sb[:], in_=ropec), writes=[cos_sb])
    kb.dma("sp", lambda h: h.dma_start(out=sin_sb[:], in_=ropes), writes=[sin_sb])
    kb.op("dve", lambda h: h.tensor_copy(out=cst_b[:], in_=cst_f[:]), reads=[cst_f], writes=[cst_b])
    kb.op("dve", lambda h: h.tensor_scalar(out=mod_sb[:, 0, :], in0=mod_sb[:, 0, :], scalar1=1.0, scalar2=None, op0=ALU.add),
          reads=[mod_sb], writes=[mod_sb])
    kb.op("dve", lambda h: h.tensor_scalar(out=mod_sb[:, 2, :], in0=mod_sb[:, 2, :], scalar1=1.0, scalar2=None, op0=ALU.add),
          reads=[mod_sb], writes=[mod_sb])
    kb.op("dve", lambda h: h.tensor_scalar(out=gq[:], in0=nrm_sb[:, 0:1], scalar1=float(128 ** -0.5), scalar2=None, op0=ALU.mult),
          reads=[nrm_sb], writes=[gq])
    ident_b = cst_b[:, 0, :]
    perm_b = cst_b[:, 1, :]
    ones_b = cst_b[:, 2, :]

    hT = kb.sb("hT", [128, 16, NTOK1], BF16)
    xt = [kb.sb("xt%d" % i, [128, D], F32) for i in range(2)]
    xn = [kb.sb("xn%d" % i, [128, D], BF16) for i in range(2)]
    st = [kb.sb("st%d" % i, [128, 4, 6], F32) for i in range(2)]
    mv = [kb.sb("mv%d" % i, [128, 2], F32) for i in range(2)]
    rs = [kb.sb("rs%d" % i, [128, 2], F32) for i in range(2)]
    pT = [kb.ps("pT%d" % i, [128, 4, 128], BF16) for i in range(2)]
    npt = 0
    for t in range(NT1):
        b = t % 2
        X = xt[b]; XN = xn[b]; ST = st[b]; MV = mv[b]; RS = rs[b]
        kb.dma("sp", lambda h, X=X, t=t: h.dma_start(out=X[:], in_=x_in[t * 128:(t + 1) * 128, :]), writes=[X])
        for j in range(4):
            kb.op("dve", lambda h, X=X, ST=ST, j=j: h.bn_stats(out=ST[:, j, :], in_=X[:, j * 512:(j + 1) * 512]), reads=[X], writes=[ST])
        kb.op("dve", lambda h, ST=ST, MV=MV: h.bn_aggr(out=MV[:], in_=ST[:].rearrange("p a b -> p (a b)")), reads=[ST], writes=[MV])
        kb.op("act", lambda h, MV=MV, RS=RS: h.activation(out=RS[:, 0:1], in_=MV[:, 1:2], func=AF.Sqrt, bias=EPS, scale=1.0), reads=[MV], writes=[RS])
        kb.op("dve", lambda h, RS=RS: h.reciprocal(out=RS[:, 0:1], in_=RS[:, 0:1]), reads=[RS], writes=[RS])
        kb.op("dve", lambda h, RS=RS, MV=MV: h.tensor_scalar(out=RS[:, 1:2], in0=MV[:, 0:1], scalar1=-1.0, scalar2=RS[:, 0:1], op0=ALU.mult, op1=ALU.mult),
              reads=[RS, MV], writes=[RS])
        kb.op("act", lambda h, X=X, XN=XN, RS=RS: h.activation(out=XN[:], in_=X[:], func=AF.Identity, scale=RS[:, 0:1], bias=RS[:, 1:2]),
              reads=[X, RS], writes=[XN])
        mi = 0 if t < 16 else 2
        for c4 in range(4):
            P = pT[npt % 2]; npt += 1
            for j in range(4):
                c = c4 * 4 + j
                kb.op("pe", lambda h, P=P, XN=XN, c=c, j=j: h.transpose(out=P[:, j, :], in_=XN[:, c * 128:(c + 1) * 128], identity=ident_b),
                      reads=[XN, cst_b], writes=[P])
            for j in range(4):
                c = c4 * 4 + j
                eng = "dve" if j % 2 == 0 else "pool"
                if eng == "pool":
                    kb.op("act", lambda h, P=P, c=c, j=j, t=t, mi=mi: h.activation(
                        out=hT[:, c, t * 128:(t + 1) * 128], in_=P[:, j, :], func=AF.Identity,
                        scale=mod_sb[:, mi, c:c + 1], bias=mod_sb[:, mi + 1, c:c + 1]), reads=[P, mod_sb], writes=[hT])
                else:
                    kb.op("dve", lambda h, P=P, c=c, j=j, t=t, mi=mi: h.tensor_scalar(
                        out=hT[:, c, t * 128:(t + 1) * 128], in0=P[:, j, :], scalar1=mod_sb[:, mi, c:c + 1],
                        scalar2=mod_sb[:, mi + 1, c:c + 1], op0=ALU.mult, op1=ALU.add), reads=[P, mod_sb], writes=[hT])

    wt = [kb.sb("wt%d" % i, [128, 16, 512], BF16) for i in range(2)]
    pacc = [kb.ps("pacc%d" % i, [128, 512], F32) for i in range(2)]
    pss = kb.ps("pss", [128, 512], F32)
    pxr = kb.ps("pxr", [128, 512], F32)
    sq = kb.sb("sq", [128, 512], BF16)
    sd = kb.sb("sd", [128, 512], F32)
    xnr = kb.sb("xnr", [128, 512], BF16)
    t1 = kb.sb("t1", [128, 512], F32)
    t2 = kb.sb("t2", [128, 512], F32)
    ob = [kb.sb("ob%d" % i, [128, 512], BF16) for i in range(3)]
    of = [kb.sb("of%d" % i, [128, 512], F32) for i in range(2)]
    nacc = 0; nob = 0; nof = 0
    TG = [(0, 512), (512, 512), (1024, 512), (1536, 512), (2048, 128)]
    for gi, (g0, gw, items) in enumerate(s1_items()):
        W = wt[gi % 2]
        kb.dma("pool", lambda h, W=W, g0=g0, gw=gw: h.dma_start(
            out=W[:, :, 0:gw], in_=w_in[:, g0:g0 + gw].rearrange("(c p) n -> p c n", p=128)), writes=[W])
        for (name, kind, c0, cw, ci) in items:
            lc = c0 - g0
            if kind == "FM":
                for (t0, tw) in TG:
                    PA = pacc[nacc % 2]; nacc += 1
                    for c in range(16):
                        kb.op("pe", lambda h, PA=PA, W=W, c=c, lc=lc, cw=cw, t0=t0, tw=tw: h.matmul(
                            PA[0:cw, 0:tw], lhsT=W[:, c, lc:lc + cw], rhs=hT[:, c, t0:t0 + tw], start=(c == 0), stop=(c == 15)),
                            reads=[W, hT], writes=[PA])
                    if name in ("qA", "kA"):
                        gcol = gq[:, 0:1] if name == "qA" else nrm_sb[:, 1:2]
                        gt = gq if name == "qA" else nrm_sb
                        kb.op("act", lambda h, PA=PA, tw=tw: h.activation(out=sq[:, 0:tw], in_=PA[:, 0:tw], func=AF.Square), reads=[PA], writes=[sq])
                        kb.op("pe", lambda h, tw=tw: h.matmul(pss[:, 0:tw], lhsT=ones_b, rhs=sq[:, 0:tw], start=True, stop=True),
                              reads=[cst_b, sq], writes=[pss])
                        kb.op("act", lambda h, tw=tw: h.activation(out=sd[:, 0:tw], in_=pss[:, 0:tw], func=AF.Sqrt, bias=EPS, scale=1.0 / 128),
                              reads=[pss], writes=[sd])
                        kb.op("dve", lambda h, tw=tw: h.reciprocal(out=sd[:, 0:tw], in_=sd[:, 0:tw]), reads=[sd], writes=[sd])
                        kb.op("dve", lambda h, PA=PA, tw=tw, gcol=gcol: h.scalar_tensor_tensor(
                            out=xnr[:, 0:tw], in0=PA[:, 0:tw], scalar=gcol, in1=sd[:, 0:tw], op0=ALU.mult, op1=ALU.mult),
                            reads=[PA, gt, sd], writes=[xnr])
                        kb.op("pe", lambda h, tw=tw: h.matmul(pxr[:, 0:tw], lhsT=perm_b, rhs=xnr[:, 0:tw], start=True, stop=True),
                              reads=[cst_b, xnr], writes=[pxr])
                        kb.op("pool", lambda h, tw=tw, t0=t0: h.tensor_tensor(out=t1[:, 0:tw], in0=xnr[:, 0:tw], in1=cos_sb[:, t0:t0 + tw], op=ALU.mult),
                              reads=[xnr, cos_sb], writes=[t1])
                        kb.op("dve", lambda h, tw=tw, t0=t0: h.tensor_tensor(out=t2[:, 0:tw], in0=pxr[:, 0:tw], in1=sin_sb[:, t0:t0 + tw], op=ALU.mult),
                              reads=[pxr, sin_sb], writes=[t2])
                        O = ob[nob % 3]; nob += 1
                        kb.op("pool", lambda h, O=O, tw=tw: h.tensor_tensor(out=O[:, 0:tw], in0=t1[:, 0:tw], in1=t2[:, 0:tw], op=ALU.add),
                              reads=[t1, t2], writes=[O])
                        dst = fm_out[name]
                        kb.dma("sp", lambda h, O=O, dst=dst, ci=ci, t0=t0, tw=tw: h.dma_start(out=dst[ci, :, t0:t0 + tw], in_=O[:, 0:tw]), reads=[O])
                    elif name in ("qB", "kB"):
                        O = ob[nob % 3]; nob += 1
                        sc = float(128 ** -0.5) if name == "qB" else 1.0
                        kb.op("act", lambda h, O=O, PA=PA, tw=tw, sc=sc: h.activation(out=O[:, 0:tw], in_=PA[:, 0:tw], func=AF.Identity, scale=sc),
                              reads=[PA], writes=[O])
                        dst = fm_out[name]
                        kb.dma("sp", lambda h, O=O, dst=dst, ci=ci, t0=t0, tw=tw: h.dma_start(out=dst[ci, :, t0:t0 + tw], in_=O[:, 0:tw]), reads=[O])
                    else:
                        O = of[nof % 2]; nof += 1
                        kb.op("dve", lambda h, O=O, PA=PA, tw=tw, cw=cw: h.tensor_copy(out=O[0:cw, 0:tw], in_=PA[0:cw, 0:tw]), reads=[PA], writes=[O])
                        if name == "cC":
                            kb.dma("sp", lambda h, O=O, ci=ci, t0=t0, tw=tw: h.dma_start(out=o_cC[ci, :, t0:t0 + tw], in_=O[:, 0:tw]), reads=[O])
                        else:
                            kb.dma("sp", lambda h, O=O, t0=t0, tw=tw: h.dma_start(out=o_ab[:, t0:t0 + tw], in_=O[0:16, 0:tw]), reads=[O])
            else:
                dst, doff = tm_out[name]
                for t in range(NT1):
                    PA = pacc[nacc % 2]; nacc += 1
                    for c in range(16):
                        kb.op("pe", lambda h, PA=PA, W=W, c=c, lc=lc, cw=cw, t=t: h.matmul(
                            PA[:, 0:cw], lhsT=hT[:, c, t * 128:(t + 1) * 128], rhs=W[:, c, lc:lc + cw], start=(c == 0), stop=(c == 15)),
                            reads=[W, hT], writes=[PA])
                    O = ob[nob % 3]; nob += 1
                    fn = AF.Silu if name[0] == "z" else AF.Identity
                    kb.op("act", lambda h, O=O, PA=PA, cw=cw, fn=fn: h.activation(out=O[:, 0:cw], in_=PA[:, 0:cw], func=fn), reads=[PA], writes=[O])
                    kb.dma("sp", lambda h, O=O, dst=dst, t=t, cw=cw, off=doff + ci: h.dma_start(
                        out=dst[t * 128:(t + 1) * 128, off:off + cw], in_=O[:, 0:cw]), reads=[O])
    print("s1 instructions:", kb.ninst)
    return kb.finish()


s1 = _NS(); s1.build_mod = build_mod; s1.build_s1 = build_s1


T = 4352
NSB = 34


def emit_A(kb, a_qT, a_kT, a_v, y_out, ident_b=None):
    kT = kb.sb("a_kT", [128, T], BF16)
    va = kb.sb("a_va", [128, NSB, 130], BF16)
    qT = [kb.sb("a_qT%d" % i, [128, T], BF16) for i in range(4)]
    kb.dma("sp", lambda h: h.dma_start(out=kT[:], in_=a_kT), writes=[kT])
    kb.op("pool", lambda h: h.memset(va[:, :, 128:130], 1.0), writes=[va])
    kb.dma("sp", lambda h: h.dma_start(out=va[:, :, 0:128], in_=a_v.rearrange("(s p) d -> p s d", p=128)), reads=[va], writes=[va])
    for i in range(4):
        kb.dma("sp", lambda h, i=i: h.dma_start(out=qT[i][:], in_=a_qT[i]), writes=[qT[i]])
    ps = [kb.ps("a_ps%d" % i, [128, 512], F32) for i in range(3)]
    po = [kb.ps("a_po%d" % i, [128, 512], F32) for i in range(4)]
    pt = [kb.sb("a_pt%d" % i, [128, 512], BF16) for i in range(3)]
    rinv = [kb.sb("a_ri%d" % i, [128, 1], F32) for i in range(2)]
    yo = [kb.sb("a_yo%d" % i, [128, 128], BF16) for i in range(3)]
    n = 0; ne = 0
    qtiles = [(0, 256, 2)] + [(256 + i * 512, 512, NSB) for i in range(8)]
    for (q0, nq, nsb) in qtiles:
        for hq in range(4):
            nqs = nq // 128
            for sb in range(nsb):
                PS = ps[n % 3]; PT = pt[n % 3]; n += 1
                kb.op("pe", lambda h, PS=PS, sb=sb, hq=hq, q0=q0, nq=nq: h.matmul(
                    PS[:, 0:nq], lhsT=kT[:, sb * 128:(sb + 1) * 128], rhs=qT[hq][:, q0:q0 + nq], start=True, stop=True),
                    reads=[kT, qT[hq]], writes=[PS])
                kb.op("act", lambda h, PS=PS, PT=PT, nq=nq: h.activation(out=PT[:, 0:nq], in_=PS[:, 0:nq], func=AF.Exp), reads=[PS], writes=[PT])
                for qs in range(nqs):
                    kb.op("pe", lambda h, PT=PT, qs=qs, sb=sb, nsb=nsb: h.matmul(
                        po[qs][:, 0:129], lhsT=PT[:, qs * 128:(qs + 1) * 128], rhs=va[:, sb, 0:129], start=(sb == 0), stop=(sb == nsb - 1)),
                        reads=[PT, va], writes=[po[qs]])
            for qs in range(nqs):
                RI = rinv[ne % 2]; YO = yo[ne % 3]; ne += 1
                kb.op("dve", lambda h, RI=RI, qs=qs: h.reciprocal(out=RI[:], in_=po[qs][:, 128:129]), reads=[po[qs]], writes=[RI])
                kb.op("dve", lambda h, RI=RI, YO=YO, qs=qs: h.tensor_scalar(out=YO[:], in0=po[qs][:, 0:128], scalar1=RI[:, 0:1], scalar2=None, op0=ALU.mult),
                      reads=[po[qs], RI], writes=[YO])
                r0 = q0 + qs * 128
                kb.dma("sp", lambda h, YO=YO, r0=r0, hq=hq: h.dma_start(out=y_out[r0:r0 + 128, hq * 128:(hq + 1) * 128], in_=YO[:]), reads=[YO])


def b_pairs():
    out = []
    for r in range(0, 64, 2):
        if r == 0:
            out.append((r, 0, 0))
        elif r == 2:
            out.append((r, 1, 0))
        elif r == 60:
            out.append((r, 3, 54))
        elif r == 62:
            out.append((r, 4, 54))
        else:
            out.append((r, 2, r - 4))
    return out


def emit_B(kb, b_qT, b_kT, b_v, b_mb, y_out, ycol0, ident_b, cst_tile):
    psA = [kb.ps("b_psA%d" % i, [128, 512], F32) for i in range(2)]
    psB = [kb.ps("b_psB%d" % i, [128, 512], F32) for i in range(2)]
    po = [kb.ps("b_po%d" % i, [128, 512], F32) for i in range(2)]
    for hd in range(2):
        kT = kb.sb("b_kT%d" % hd, [128, T], BF16)
        qT = kb.sb("b_qT%d" % hd, [128, T], BF16)
        va = kb.sb("b_va%d" % hd, [128, NSB, 130], BF16)
        mbf = kb.sb("b_mbf%d" % hd, [128, 25 * 128], F32)
        mb = kb.sb("b_mb%d" % hd, [128, 5, 5, 128], BF16)
        kb.dma("sp", lambda h, kT=kT, hd=hd: h.dma_start(out=kT[:], in_=b_kT[hd]), writes=[kT])
        kb.dma("sp", lambda h, qT=qT, hd=hd: h.dma_start(out=qT[:], in_=b_qT[hd]), writes=[qT])
        kb.op("pool", lambda h, va=va: h.memset(va[:, :, 128:130], 1.0), writes=[va])
        kb.dma("sp", lambda h, va=va, hd=hd: h.dma_start(out=va[:, :, 0:128], in_=b_v[hd].rearrange("(s p) d -> p s d", p=128)), reads=[va], writes=[va])
        kb.dma("sp", lambda h, mbf=mbf, hd=hd: h.dma_start(out=mbf[:], in_=b_mb[hd].rearrange("p a b q -> p (a b q)")), writes=[mbf])
        kb.op("dve", lambda h, mbf=mbf, mb=mb: h.tensor_copy(out=mb[:].rearrange("p a b q -> p (a b q)"), in_=mbf[:]), reads=[mbf], writes=[mb])
        pt = [kb.sb("b_pt%d_%d" % (hd, i), [128, 7, 128], BF16) for i in range(2)]
        rinv = [kb.sb("b_ri%d_%d" % (hd, i), [128, 1], F32) for i in range(2)]
        yo = [kb.sb("b_yo%d_%d" % (hd, i), [128, 128], BF16) for i in range(2)]
        n = 0
        units = [("ctx", 0, None, None), ("ctx", 128, None, None)] + [("lat", 256 + r * 64, case, base) for (r, case, base) in b_pairs()]
        for (kind, q0, case, base) in units:
            PSa = psA[n % 2]; PSb = psB[n % 2]; PO = po[n % 2]; PT = pt[n % 2]; RI = rinv[n % 2]; YO = yo[n % 2]; n += 1
            blocks = [0, 1]
            if kind == "lat":
                blocks = blocks + [2 + base // 2 + j for j in range(5)]
            nb = len(blocks)
            for bi, blk in enumerate(blocks):
                lat = bi >= 2
                PS = PSa if bi < 4 else PSb
                kb.op("pe", lambda h, PS=PS, bi=bi, blk=blk, q0=q0, lat=lat, kT=kT, qT=qT: h.matmul(
                    PS[:, (bi % 4) * 128:(bi % 4 + 1) * 128], lhsT=kT[:, blk * 128:(blk + 1) * 128], rhs=qT[:, q0:q0 + 128], start=True, stop=(not lat)),
                    reads=[kT, qT], writes=[PS])
                if lat:
                    kb.op("pe", lambda h, PS=PS, bi=bi, case=case, mb=mb: h.matmul(
                        PS[:, (bi % 4) * 128:(bi % 4 + 1) * 128], lhsT=ident_b, rhs=mb[:, case, bi - 2, :], start=False, stop=True),
                        reads=[mb, cst_tile], writes=[PS])
            kb.op("act", lambda h, PSa=PSa, PT=PT, nb=nb: h.activation(
                out=PT[:, 0:min(nb, 4), :], in_=PSa[:, 0:min(nb, 4) * 128].rearrange("p (a q) -> p a q", q=128), func=AF.Exp), reads=[PSa], writes=[PT])
            if nb > 4:
                kb.op("act", lambda h, PSb=PSb, PT=PT, nb=nb: h.activation(
                    out=PT[:, 4:nb, :], in_=PSb[:, 0:(nb - 4) * 128].rearrange("p (a q) -> p a q", q=128), func=AF.Exp), reads=[PSb], writes=[PT])
            for bi, blk in enumerate(blocks):
                kb.op("pe", lambda h, PO=PO, PT=PT, bi=bi, blk=blk, nb=nb, va=va: h.matmul(
                    PO[:, 0:129], lhsT=PT[:, bi, :], rhs=va[:, blk, 0:129], start=(bi == 0), stop=(bi == nb - 1)),
                    reads=[PT, va], writes=[PO])
            kb.op("dve", lambda h, RI=RI, PO=PO: h.reciprocal(out=RI[:], in_=PO[:, 128:129]), reads=[PO], writes=[RI])
            kb.op("dve", lambda h, RI=RI, YO=YO, PO=PO: h.tensor_scalar(out=YO[:], in0=PO[:, 0:128], scalar1=RI[:, 0:1], scalar2=None, op0=ALU.mult),
                  reads=[PO, RI], writes=[YO])
            c0 = ycol0 + hd * 128
            kb.dma("sp", lambda h, YO=YO, q0=q0, c0=c0: h.dma_start(out=y_out[q0:q0 + 128, c0:c0 + 128], in_=YO[:]), reads=[YO])


def build_s2a():
    kb = KB()
    a_qT = kb.dram("a_qT", [4, 128, T], BF16, "ExternalInput")
    a_kT = kb.dram("a_kT", [128, T], BF16, "ExternalInput")
    a_v = kb.dram("a_v", [T, 128], BF16, "ExternalInput")
    y = kb.dram("y", [T, 512], BF16, "ExternalOutput")
    emit_A(kb, a_qT, a_kT, a_v, y)
    print("s2a instructions:", kb.ninst)
    return kb.finish()


def build_s2b():
    kb = KB()
    b_qT = kb.dram("b_qT", [2, 128, T], BF16, "ExternalInput")
    b_kT = kb.dram("b_kT", [2, 128, T], BF16, "ExternalInput")
    b_v = kb.dram("b_v", [2, T, 128], BF16, "ExternalInput")
    b_mb = kb.dram("b_mb", [2, 128, 5, 5, 128], F32, "ExternalInput")
    idn = kb.dram("idn", [128, 128], F32, "ExternalInput")
    y = kb.dram("y", [T, 256], BF16, "ExternalOutput")
    idf = kb.sb("idf", [128, 128], F32)
    idb = kb.sb("idb", [128, 128], BF16)
    kb.dma("sp", lambda h: h.dma_start(out=idf[:], in_=idn), writes=[idf])
    kb.op("dve", lambda h: h.tensor_copy(out=idb[:], in_=idf[:]), reads=[idf], writes=[idb])
    emit_B(kb, b_qT, b_kT, b_v, b_mb, y, 0, idb[:], idb)
    print("s2b instructions:", kb.ninst)
    return kb.finish()


s2 = _NS(); s2.build_s2a = build_s2a; s2.build_s2b = build_s2b


CSTOP = int(os.environ.get("CSTOP", "99"))
CSUB = int(os.environ.get("CSUB", "99"))
CFIN = int(os.environ.get("CFIN", "1"))

T = 4352
NCH = 34
EPS = 1e-6
NEG = -30000.0
NHD = 1


def build_s2c():
    kb = KB()
    c_qkv = kb.dram("c_qkv", [3, 128, T], F32, "ExternalInput")
    c_ab = kb.dram("c_ab", [128, NCH * 16], F32, "ExternalInput")
    c_cw = kb.dram("c_cw", [128, 3, 3], F32, "ExternalInput")
    c_gp = kb.dram("c_gp", [128, 2, NCH * 16], F32, "ExternalInput")
    c_on = kb.dram("c_on", [128, 128], F32, "ExternalInput")
    c_cst = kb.dram("c_cst", [128, 15, 128], F32, "ExternalInput")
    y = kb.dram("y", [T, 128], BF16, "ExternalOutput")

    cst = kb.sb("cst", [128, 15, 128], F32)
    kb.dma("sp", lambda h: h.dma_start(out=cst[:], in_=c_cst), writes=[cst])
    ident = cst[:, 0, :]; ones = cst[:, 1, :]
    onrm = kb.sb("onrm", [128, 128], F32)
    kb.dma("sp", lambda h: h.dma_start(out=onrm[:], in_=c_on), writes=[onrm])
    cw = kb.sb("cw", [128, 3, 3], F32)
    kb.dma("sp", lambda h: h.dma_start(out=cw[:], in_=c_cw), writes=[cw])

    W = NCH * 16
    ab = kb.sb("ab", [128, W], F32)
    gp = kb.sb("gp", [128, 2, W], F32)
    kb.dma("sp", lambda h: h.dma_start(out=ab[:], in_=c_ab), writes=[ab])
    kb.dma("sp", lambda h: h.dma_start(out=gp[:], in_=c_gp), writes=[gp])
    beta = kb.sb("beta", [128, W], F32)
    lnb = kb.sb("lnb", [128, W], F32)
    g = kb.sb("g", [128, W], F32)
    tmp = kb.sb("gtmp", [128, W], F32)
    ea = kb.sb("ea", [128, W], F32)
    cF = kb.sb("cF", [128, W], F32)
    cB = kb.sb("cB", [128, W], F32)
    kb.op("act", lambda h: h.activation(out=beta[:], in_=ab[:], func=AF.Sigmoid), reads=[ab], writes=[beta])
    kb.op("act", lambda h: h.activation(out=lnb[:], in_=beta[:], func=AF.Ln), reads=[beta], writes=[lnb])
    kb.op("dve", lambda h: h.tensor_tensor(out=tmp[:], in0=ab[:], in1=gp[:, 1, :], op=ALU.add), reads=[ab, gp], writes=[tmp])
    kb.op("act", lambda h: h.activation(out=tmp[:], in_=tmp[:], func=AF.Exp), reads=[tmp], writes=[tmp])
    kb.op("act", lambda h: h.activation(out=tmp[:], in_=tmp[:], func=AF.Ln, bias=1.0, scale=1.0), reads=[tmp], writes=[tmp])
    kb.op("act", lambda h: h.activation(out=ea[:], in_=gp[:, 0, :], func=AF.Exp), reads=[gp], writes=[ea])
    kb.op("dve", lambda h: h.scalar_tensor_tensor(out=g[:], in0=tmp[:], scalar=-1.0, in1=ea[:], op0=ALU.mult, op1=ALU.mult), reads=[tmp, ea], writes=[g])
    PS8 = [kb.ps("ps%d" % i, [128, 512], F32) for i in range(8)]
    pg = [PS8[0], PS8[4]]
    for di, (tri, dst) in enumerate([(2, cF), (3, cB)]):
        for (n0, nw) in [(0, 512), (512, W - 512)]:
            P = pg[(di * 2 + (n0 > 0)) % 2]
            kb.op("pe", lambda h, P=P, tri=tri, n0=n0, nw=nw: h.matmul(P[:, 0:nw], lhsT=cst[:, tri, :], rhs=g[:, n0:n0 + nw], start=True, stop=True),
                  reads=[cst, g], writes=[P])
            kb.op("dve", lambda h, P=P, dst=dst, n0=n0, nw=nw: h.tensor_copy(out=dst[:, n0:n0 + nw], in_=P[:, 0:nw]), reads=[P], writes=[dst])
    def v4(t):
        return t[:].rearrange("p (c d k h) -> p c d k h", d=2, k=2, h=4)
    GC = kb.sb("GC", [128, NCH, 2, 4], F32)
    NGC = kb.sb("NGC", [128, NCH, 2, 4], F32)
    GL = kb.sb("GL", [128, NCH, 2, 4], F32)
    BE = kb.sb("BE", [128, NCH, 2, 4], F32)
    BEK = kb.sb("BEK", [128, NCH, 2, 4], F32)
    ER = kb.sb("ER", [128, NCH, 2, 4], F32)
    kb.op("dve", lambda h: h.tensor_copy(out=GC[:, :, 0, :], in_=v4(cF)[:, :, 0, 1, :]), reads=[cF], writes=[GC])
    kb.op("dve", lambda h: h.tensor_copy(out=GC[:, :, 1, :], in_=v4(cB)[:, :, 1, 1, :]), reads=[cB], writes=[GC])
    kb.op("dve", lambda h: h.tensor_scalar(out=NGC[:], in0=GC[:], scalar1=-1.0, scalar2=None, op0=ALU.mult), reads=[GC], writes=[NGC])
    kb.op("dve", lambda h: h.tensor_tensor(out=GL[:], in0=GC[:], in1=v4(lnb)[:, :, :, 0, :], op=ALU.add), reads=[GC, lnb], writes=[GL])
    kb.op("dve", lambda h: h.tensor_copy(out=BE[:], in_=v4(beta)[:, :, :, 0, :]), reads=[beta], writes=[BE])
    kb.op("act", lambda h: h.activation(out=BEK[:], in_=GL[:], func=AF.Exp), reads=[GL], writes=[BEK])
    kb.op("dve", lambda h: h.tensor_tensor(out=ER[:, :, 0, :], in0=v4(cB)[:, :, 0, 1, :], in1=v4(g)[:, :, 0, 1, :], op=ALU.subtract), reads=[cB, g], writes=[ER])
    kb.op("dve", lambda h: h.tensor_tensor(out=ER[:, :, 1, :], in0=v4(cF)[:, :, 1, 1, :], in1=v4(g)[:, :, 1, 1, :], op=ALU.subtract), reads=[cF, g], writes=[ER])
    kb.op("act", lambda h: h.activation(out=ER[:], in_=ER[:], func=AF.Exp), reads=[ER], writes=[ER])

    if CSTOP <= 1:
        return kb.finish()
    xr = [kb.sb("xr%d" % i, [128, T], F32) for i in range(1)]
    xc = kb.sb("xc", [128, T], F32)
    QKV = [[kb.sb("qkv%d_%d" % (k, hd), [128, T], F32) for hd in range(NHD)] for k in range(3)]
    sq = kb.sb("sq", [128, 512], F32)
    sd = kb.sb("sd", [128, 512], F32)
    segs = [(0, 256), (256, T)]
    for k in range(3):
        for hd in range(NHD):
            idx = k
            X = xr[0]; Y = QKV[k][hd]
            kb.dma("sp", lambda h, X=X, idx=idx: h.dma_start(out=X[:], in_=c_qkv[idx]), writes=[X])
            kb.op("dve", lambda h, X=X, idx=idx: h.tensor_scalar(out=xc[:], in0=X[:], scalar1=cw[:, idx, 1:2], scalar2=None, op0=ALU.mult),
                  reads=[X, cw], writes=[xc])
            for (s0, s1) in segs:
                kb.op("dve", lambda h, X=X, idx=idx, s0=s0, s1=s1: h.scalar_tensor_tensor(
                    out=xc[:, s0 + 1:s1], in0=X[:, s0:s1 - 1], scalar=cw[:, idx, 0:1], in1=xc[:, s0 + 1:s1], op0=ALU.mult, op1=ALU.add),
                    reads=[X, cw, xc], writes=[xc])
                kb.op("dve", lambda h, X=X, idx=idx, s0=s0, s1=s1: h.scalar_tensor_tensor(
                    out=xc[:, s0:s1 - 1], in0=X[:, s0 + 1:s1], scalar=cw[:, idx, 2:3], in1=xc[:, s0:s1 - 1], op0=ALU.mult, op1=ALU.add),
                    reads=[X, cw, xc], writes=[xc])
            kb.op("act", lambda h, Y=Y: h.activation(out=Y[:], in_=xc[:], func=AF.Silu), reads=[xc], writes=[Y])
            if k < 2:
                sc = float(128 ** -0.5) if k == 0 else 1.0
                for n0 in range(0, T, 512):
                    nw = min(512, T - n0)
                    P = pg[(n0 // 512) % 2]
                    kb.op("act", lambda h, Y=Y, n0=n0, nw=nw: h.activation(out=sq[:, 0:nw], in_=Y[:, n0:n0 + nw], func=AF.Square), reads=[Y], writes=[sq])
                    kb.op("pe", lambda h, P=P, nw=nw: h.matmul(P[:, 0:nw], lhsT=ones, rhs=sq[:, 0:nw], start=True, stop=True), reads=[cst, sq], writes=[P])
                    kb.op("act", lambda h, P=P, nw=nw: h.activation(out=sd[:, 0:nw], in_=P[:, 0:nw], func=AF.Sqrt, bias=EPS, scale=1.0), reads=[P], writes=[sd])
                    kb.op("dve", lambda h, nw=nw: h.reciprocal(out=sd[:, 0:nw], in_=sd[:, 0:nw]), reads=[sd], writes=[sd])
                    kb.op("dve", lambda h, Y=Y, n0=n0, nw=nw, sc=sc: h.scalar_tensor_tensor(
                        out=Y[:, n0:n0 + nw], in0=Y[:, n0:n0 + nw], scalar=sc, in1=sd[:, 0:nw], op0=ALU.mult, op1=ALU.mult), reads=[Y, sd], writes=[Y])

    if CSTOP <= 2:
        return kb.finish()
    chains = [(hd, d) for hd in range(NHD) for d in range(2)]
    CH = {}
    for ci, (hd, d) in enumerate(chains):
        c = {}
        c["pKQ"] = PS8[4 * ci]
        c["pMN"] = PS8[4 * ci + 1]
        c["pBC"] = PS8[4 * ci + 2]
        c["pP"] = PS8[4 * ci + 3]
        for nm, shp in [("dg", [128, 3, 128]), ("dec", [128, 256]), ("egc", [128, 128]), ("QKT", [128, 128]), ("qTg", [128, 128]),
                        ("MN0", [128, 2, 128]), ("MK0", [128, 2, 128]), ("MK1", [128, 2, 128]), ("XS", [128, 2, 128]), ("PQ0", [128, 2, 128]), ("PQ1", [128, 2, 128]),
                        ("bv", [128, 128]), ("kbe", [128, 128]), ("kr", [128, 128]), ("u", [128, 128]), ("wT", [128, 128]),
                        ("vnew", [128, 128]), ("S", [128, 128])]:
            c[nm] = kb.sb("%s_%d" % (nm, ci), shp, F32)
        CH[(hd, d)] = c
        kb.op("pool", lambda h, c=c: h.memset(c["S"][:], 0.0), writes=[c["S"]])
    O = [kb.sb("O%d" % hd, [128, NCH, 128], F32) for hd in range(NHD)]
    order = {0: list(range(NCH)), 1: [1, 0] + list(range(NCH - 1, 1, -1))}
    seenO = set()
    for step in range(NCH if CSTOP > 10 else CSTOP - 2):
        for hd in range(NHD):
            for d in range(2):
                ch = order[d][step]
                c = CH[(hd, d)]
                q = d * 4
                cs = slice(ch * 128, (ch + 1) * 128)
                kT = QKV[1][hd]; vT = QKV[2][hd]; qT = QKV[0][hd]
                if CSUB <= -2:
                    continue
                kb.op("pe", lambda h, c=c, kT=kT, cs=cs: h.transpose(out=c["pMN"][:, 0:128], in_=kT[:, cs], identity=ident), reads=[kT, cst], writes=[c["pMN"]])
                if CSUB <= -1:
                    continue
                kb.op("pe", lambda h, c=c, vT=vT, cs=cs: h.transpose(out=c["pMN"][:, 128:256], in_=vT[:, cs], identity=ident), reads=[vT, cst], writes=[c["pMN"]])
                hh = hd
                col = lambda TT, ch=ch, d=d, hh=hh: TT[:, ch, d, hh:hh + 1]
                kb.op("dve", lambda h, c=c, col=col: h.tensor_scalar(out=c["kbe"][:], in0=c["pMN"][:, 0:128], scalar1=col(BEK), scalar2=None, op0=ALU.mult),
                      reads=[c["pMN"], BEK], writes=[c["kbe"]])
                if CSUB <= 0:
                    continue
                kb.op("dve", lambda h, c=c, col=col: h.tensor_scalar(out=c["kr"][:], in0=c["pMN"][:, 0:128], scalar1=col(ER), scalar2=None, op0=ALU.mult),
                      reads=[c["pMN"], ER], writes=[c["kr"]])
                kb.op("dve", lambda h, c=c, col=col: h.tensor_scalar(out=c["bv"][:], in0=c["pMN"][:, 128:256], scalar1=col(BE), scalar2=None, op0=ALU.mult),
                      reads=[c["pMN"], BE], writes=[c["bv"]])
                if CSUB <= 1:
                    continue
                kb.op("pe", lambda h, c=c, kT=kT, qT=qT, cs=cs: h.matmul(c["pKQ"][:, 0:128], lhsT=kT[:, cs], rhs=qT[:, cs], start=True, stop=True),
                      reads=[kT, qT], writes=[c["pKQ"]])
                kb.op("pe", lambda h, c=c, kT=kT, cs=cs: h.matmul(c["pKQ"][:, 128:256], lhsT=kT[:, cs], rhs=kT[:, cs], start=True, stop=True),
                      reads=[kT], writes=[c["pKQ"]])
                if CSUB <= 2:
                    continue
                kb.op("pool", lambda h, c=c, col=col: h.tensor_scalar(out=c["dg"][:, 0, :], in0=ident, scalar1=col(GC), scalar2=None, op0=ALU.mult),
                      reads=[cst, GC], writes=[c["dg"]])
                kb.op("pool", lambda h, c=c, col=col: h.tensor_scalar(out=c["dg"][:, 1, :], in0=ident, scalar1=col(GL), scalar2=None, op0=ALU.mult),
                      reads=[cst, GL], writes=[c["dg"]])
                kb.op("pool", lambda h, c=c, col=col: h.tensor_scalar(out=c["dg"][:, 2, :], in0=ident, scalar1=col(GC), scalar2=None, op0=ALU.mult),
                      reads=[cst, GC], writes=[c["dg"]])
                if CSUB <= 3:
                    continue
                mi = 4 if d == 0 else 6
                kb.op("pe", lambda h, c=c: h.matmul(c["pBC"][:, 0:384], lhsT=ones, rhs=c["dg"][:].rearrange("p a b -> p (a b)"), start=True, stop=False),
                      reads=[cst, c["dg"]], writes=[c["pBC"]])
                kb.op("pe", lambda h, c=c, mi=mi: h.matmul(c["pBC"][:, 0:256], lhsT=ident, rhs=cst[:, mi:mi + 2, :].rearrange("p a b -> p (a b)"), start=False, stop=True),
                      reads=[cst], writes=[c["pBC"]])
                if CSUB <= 4:
                    continue
                kb.op("act", lambda h, c=c, col=col: h.activation(out=c["dec"][:], in_=c["pBC"][:, 0:256], func=AF.Exp, bias=col(NGC), scale=1.0),
                      reads=[c["pBC"], NGC], writes=[c["dec"]])
                kb.op("act", lambda h, c=c: h.activation(out=c["egc"][:], in_=c["pBC"][:, 256:384], func=AF.Exp), reads=[c["pBC"]], writes=[c["egc"]])
                if CSUB <= 5:
                    continue
                kb.op("dve", lambda h, c=c: h.tensor_tensor(out=c["QKT"][:], in0=c["pKQ"][:, 0:128], in1=c["dec"][:, 0:128], op=ALU.mult),
                      reads=[c["pKQ"], c["dec"]], writes=[c["QKT"]])
                kb.op("dve", lambda h, c=c: h.scalar_tensor_tensor(out=c["MN0"][:, 0, :], in0=c["pKQ"][:, 128:256], scalar=-1.0, in1=c["dec"][:, 128:256],
                                                                    op0=ALU.mult, op1=ALU.mult), reads=[c["pKQ"], c["dec"]], writes=[c["MN0"]])
                kb.op("pool", lambda h, c=c, qT=qT, cs=cs: h.tensor_tensor(out=c["qTg"][:], in0=qT[:, cs], in1=c["egc"][:], op=ALU.mult),
                      reads=[qT, c["egc"]], writes=[c["qTg"]])
        for (hd, d) in chains:
            c = CH[(hd, d)]
            kb.op("pe", lambda h, c=c: h.transpose(out=c["pMN"][:, 128:256], in_=c["MN0"][:, 0, :], identity=ident), reads=[c["MN0"], cst], writes=[c["pMN"]])
            kb.op("act", lambda h, c=c: h.copy(out=c["MN0"][:, 1, :], in_=c["pMN"][:, 128:256]), reads=[c["pMN"]], writes=[c["MN0"]])
            kb.op("pool", lambda h, c=c: h.tensor_tensor(out=c["PQ0"][:, 0, :], in0=c["MN0"][:, 0, :], in1=cst[:, 8, :], op=ALU.mult), reads=[c["MN0"], cst], writes=[c["PQ0"]])
            kb.op("pool", lambda h, c=c: h.tensor_tensor(out=c["PQ0"][:, 1, :], in0=c["MN0"][:, 1, :], in1=cst[:, 8, :], op=ALU.mult), reads=[c["MN0"], cst], writes=[c["PQ0"]])
            kb.op("pool", lambda h, c=c: h.tensor_tensor(out=c["PQ0"][:, 0, :], in0=c["PQ0"][:, 0, :], in1=ident, op=ALU.add), reads=[cst], writes=[c["PQ0"]])
            kb.op("pool", lambda h, c=c: h.tensor_tensor(out=c["PQ0"][:, 1, :], in0=c["PQ0"][:, 1, :], in1=ident, op=ALU.add), reads=[cst], writes=[c["PQ0"]])
        for lv in range(1, 7):
            for (hd, d) in chains:
                c = CH[(hd, d)]
                cur = c["PQ%d" % ((lv - 1) % 2)]; nxt = c["PQ%d" % (lv % 2)]
                MK = c["MK%d" % (lv % 2)]
                kb.op("pool", lambda h, c=c, MK=MK, lv=lv: h.tensor_tensor(out=MK[:, 0, :], in0=c["MN0"][:, 0, :], in1=cst[:, 8 + lv, :], op=ALU.mult),
                      reads=[c["MN0"], cst], writes=[MK])
                kb.op("pool", lambda h, c=c, MK=MK, lv=lv: h.tensor_tensor(out=MK[:, 1, :], in0=c["MN0"][:, 1, :], in1=cst[:, 8 + lv, :], op=ALU.mult),
                      reads=[c["MN0"], cst], writes=[MK])
                kb.op("pe", lambda h, c=c, MK=MK, cur=cur: h.matmul(c["pMN"][:, 0:128], lhsT=MK[:, 1, :], rhs=cur[:, 0, :], start=True, stop=True),
                      reads=[MK, cur], writes=[c["pMN"]])
                kb.op("pe", lambda h, c=c, MK=MK, cur=cur: h.matmul(c["pMN"][:, 128:256], lhsT=MK[:, 0, :], rhs=cur[:, 1, :], start=True, stop=True),
                      reads=[MK, cur], writes=[c["pMN"]])
                kb.op("act", lambda h, c=c: h.copy(out=c["XS"][:].rearrange("p a b -> p (a b)"), in_=c["pMN"][:, 0:256]), reads=[c["pMN"]], writes=[c["XS"]])
                kb.op("pe", lambda h, c=c, cur=cur: h.matmul(c["pP"][:, 0:128], lhsT=cur[:, 1, :], rhs=c["XS"][:, 0, :], start=True, stop=True),
                      reads=[cur, c["XS"]], writes=[c["pP"]])
                kb.op("pe", lambda h, c=c, cur=cur: h.matmul(c["pP"][:, 128:256], lhsT=cur[:, 0, :], rhs=c["XS"][:, 1, :], start=True, stop=True),
                      reads=[cur, c["XS"]], writes=[c["pP"]])
                kb.op("dve", lambda h, c=c, cur=cur, nxt=nxt: h.tensor_tensor(out=nxt[:].rearrange("p a b -> p (a b)"), in0=c["pP"][:, 0:256],
                                                                           in1=cur[:].rearrange("p a b -> p (a b)"), op=ALU.add),
                      reads=[c["pP"], cur], writes=[nxt])
        if CSUB <= 8:
            continue
        for (hd, d) in chains:
            c = CH[(hd, d)]
            Pf = c["PQ0"]
            kb.op("pe", lambda h, c=c, Pf=Pf: h.matmul(c["pKQ"][:, 0:128], lhsT=Pf[:, 0, :], rhs=c["bv"][:], start=True, stop=True), reads=[Pf, c["bv"]], writes=[c["pKQ"]])
            kb.op("pe", lambda h, c=c, Pf=Pf: h.matmul(c["pKQ"][:, 128:256], lhsT=c["kbe"][:], rhs=Pf[:, 0, :], start=True, stop=True), reads=[Pf, c["kbe"]], writes=[c["pKQ"]])
            kb.op("act", lambda h, c=c: h.copy(out=c["u"][:], in_=c["pKQ"][:, 0:128]), reads=[c["pKQ"]], writes=[c["u"]])
            kb.op("act", lambda h, c=c: h.copy(out=c["wT"][:], in_=c["pKQ"][:, 128:256]), reads=[c["pKQ"]], writes=[c["wT"]])
        for (hd, d) in chains:
            c = CH[(hd, d)]
            ch = order[d][step]
            last = 127 if d == 0 else 0
            kb.op("pe", lambda h, c=c: h.matmul(c["pMN"][:, 0:128], lhsT=c["wT"][:], rhs=c["S"][:], start=True, stop=True), reads=[c["wT"], c["S"]], writes=[c["pMN"]])
            kb.op("dve", lambda h, c=c: h.tensor_tensor(out=c["vnew"][:], in0=c["u"][:], in1=c["pMN"][:, 0:128], op=ALU.subtract),
                  reads=[c["u"], c["pMN"]], writes=[c["vnew"]])
            kb.op("pe", lambda h, c=c: h.matmul(c["pP"][:, 0:128], lhsT=c["qTg"][:], rhs=c["S"][:], start=True, stop=False), reads=[c["qTg"], c["S"]], writes=[c["pP"]])
            kb.op("pe", lambda h, c=c: h.matmul(c["pP"][:, 0:128], lhsT=c["QKT"][:], rhs=c["vnew"][:], start=False, stop=True), reads=[c["QKT"], c["vnew"]], writes=[c["pP"]])
            kb.op("pe", lambda h, c=c: h.matmul(c["pMN"][:, 128:256], lhsT=c["kr"][:], rhs=c["vnew"][:], start=True, stop=True), reads=[c["kr"], c["vnew"]], writes=[c["pMN"]])
            if (hd, ch) not in seenO:
                seenO.add((hd, ch))
                kb.op("act", lambda h, c=c, hd=hd, ch=ch: h.copy(out=O[hd][:, ch, :], in_=c["pP"][:, 0:128]), reads=[c["pP"]], writes=[O[hd]])
            else:
                kb.op("dve", lambda h, c=c, hd=hd, ch=ch: h.tensor_tensor(out=O[hd][:, ch, :], in0=c["pP"][:, 0:128], in1=O[hd][:, ch, :], op=ALU.add),
                      reads=[c["pP"], O[hd]], writes=[O[hd]])
            kb.op("dve", lambda h, c=c, last=last: h.scalar_tensor_tensor(out=c["S"][:], in0=c["S"][:], scalar=c["egc"][:, last:last + 1], in1=c["pMN"][:, 128:256],
                                                                          op0=ALU.mult, op1=ALU.add), reads=[c["S"], c["egc"], c["pMN"]], writes=[c["S"]])
    if not CFIN:
        return kb.finish()
    ss = [kb.sb("ss%d" % i, [128, 2], F32) for i in range(2)]
    junk = kb.sb("junk", [128, 128], F32)
    yo = [kb.sb("yo%d" % i, [128, 128], BF16) for i in range(2)]
    n = 0
    for hd in range(NHD):
        for ch in range(NCH):
            SS = ss[n % 2]; YO = yo[n % 2]; n += 1
            kb.op("act", lambda h, hd=hd, ch=ch: h.activation(out=junk[:], in_=O[hd][:, ch, :], func=AF.Square), reads=[O[hd]], writes=[junk])
            kb.op("dve", lambda h, SS=SS: h.tensor_reduce(out=SS[:, 0:1], in_=junk[:], axis=AX.X, op=ALU.add), reads=[junk], writes=[SS])
            kb.op("act", lambda h, SS=SS: h.activation(out=SS[:, 1:2], in_=SS[:, 0:1], func=AF.Sqrt, bias=EPS, scale=1.0 / 128), reads=[SS], writes=[SS])
            kb.op("dve", lambda h, SS=SS: h.reciprocal(out=SS[:, 1:2], in_=SS[:, 1:2]), reads=[SS], writes=[SS])
            kb.op("dve", lambda h, SS=SS, YO=YO, hd=hd, ch=ch: h.scalar_tensor_tensor(out=YO[:], in0=O[hd][:, ch, :], scalar=SS[:, 1:2], in1=onrm[:],
                                                                                   op0=ALU.mult, op1=ALU.mult), reads=[O[hd], SS, onrm], writes=[YO])
            kb.dma("sp", lambda h, YO=YO, hd=hd, ch=ch: h.dma_start(out=y[ch * 128:(ch + 1) * 128, hd * 128:(hd + 1) * 128], in_=YO[:]), reads=[YO])
    print("s2c instructions:", kb.ninst)
    return kb.finish()


s2c = _NS(); s2c.build_s2c = build_s2c


D = 2048
NT1 = 17
NTOK1 = NT1 * 128
EPS = 1e-6
ALPHA = (2 * 2) ** 0.25


def build_s3():
    kb = KB()
    ycat = kb.dram("ycat", [NTOK1, D], BF16, "ExternalInput")
    sz = kb.dram("sz", [NTOK1, D], BF16, "ExternalInput")
    x_in = kb.dram("x_in", [NTOK1, D], F32, "ExternalInput")
    w_out = kb.dram("w_out", [D, D], F32, "ExternalInput")
    rows = kb.dram("rows", [128, 4, D], F32, "ExternalInput")
    idn = kb.dram("idn", [128, 128], F32, "ExternalInput")
    x_out = kb.dram("x_out", [NTOK1, D], F32, "ExternalOutput")

    idf = kb.sb("idf", [128, 128], F32)
    idb = kb.sb("idb", [128, 128], BF16)
    kb.dma("sp", lambda h: h.dma_start(out=idf[:], in_=idn), writes=[idf])
    kb.op("dve", lambda h: h.tensor_copy(out=idb[:], in_=idf[:]), reads=[idf], writes=[idb])
    rw = kb.sb("rw", [128, 4, D], F32)
    kb.dma("sp", lambda h: h.dma_start(out=rw[:], in_=rows), writes=[rw])
    wo = [kb.sb("wo%d" % n, [128, 16, 512], BF16) for n in range(4)]
    for n in range(4):
        kb.dma("pool", lambda h, n=n: h.dma_start(out=wo[n][:], in_=w_out[:, n * 512:(n + 1) * 512].rearrange("(c p) n -> p c n", p=128)), writes=[wo[n]])
    yt = [kb.sb("yt%d" % i, [128, D], BF16) for i in range(2)]
    zt = [kb.sb("zt%d" % i, [128, D], BF16) for i in range(2)]
    xt = [kb.sb("xt%d" % i, [128, D], F32) for i in range(2)]
    yg = [kb.sb("yg%d" % i, [128, D], BF16) for i in range(2)]
    ygT = [kb.sb("ygT%d" % i, [128, 16, 128], BF16) for i in range(2)]
    r = [kb.sb("r%d" % i, [128, D], F32) for i in range(2)]
    st = [kb.sb("st%d" % i, [128, 4, 6], F32) for i in range(2)]
    mv = [kb.sb("mv%d" % i, [128, 2], F32) for i in range(2)]
    rs = [kb.sb("rs%d" % i, [128, 2], F32) for i in range(2)]
    pT = [kb.ps("pT%d" % i, [128, 4, 128], BF16) for i in range(2)]
    pacc = [kb.ps("pacc%d" % i, [128, 512], F32) for i in range(4)]
    npt = 0; nacc = 0
    for t in range(NT1):
        b = t % 2
        Y = yt[b]; Z = zt[b]; X = xt[b]; YG = yg[b]; YT = ygT[b]; R = r[b]; OT = r[b]; ST = st[b]; MV = mv[b]; RS = rs[b]
        rsl = slice(t * 128, (t + 1) * 128)
        gi = 0 if t < 16 else 1
        kb.dma("sp", lambda h, Y=Y, rsl=rsl: h.dma_start(out=Y[:], in_=ycat[rsl, :]), writes=[Y])
        kb.dma("sp", lambda h, Z=Z, rsl=rsl: h.dma_start(out=Z[:], in_=sz[rsl, :]), writes=[Z])
        kb.dma("sp", lambda h, X=X, rsl=rsl: h.dma_start(out=X[:], in_=x_in[rsl, :]), writes=[X])
        kb.op("pool", lambda h, Y=Y, Z=Z, YG=YG: h.tensor_tensor(out=YG[:], in0=Y[:], in1=Z[:], op=ALU.mult), reads=[Y, Z], writes=[YG])
        for c4 in range(4):
            P = pT[npt % 2]; npt += 1
            for j in range(4):
                c = c4 * 4 + j
                kb.op("pe", lambda h, P=P, YG=YG, c=c, j=j: h.transpose(out=P[:, j, :], in_=YG[:, c * 128:(c + 1) * 128], identity=idb[:]),
                      reads=[YG, idb], writes=[P])
            eng = "dve" if c4 % 2 == 0 else "act"
            if eng == "dve":
                kb.op("dve", lambda h, P=P, YT=YT, c4=c4: h.tensor_copy(out=YT[:, c4 * 4:(c4 + 1) * 4, :], in_=P[:]), reads=[P], writes=[YT])
            else:
                kb.op("act", lambda h, P=P, YT=YT, c4=c4: h.copy(out=YT[:, c4 * 4:(c4 + 1) * 4, :], in_=P[:]), reads=[P], writes=[YT])
        for n in range(4):
            PA = pacc[nacc % 4]; nacc += 1
            for c in range(16):
                kb.op("pe", lambda h, PA=PA, YT=YT, c=c, n=n: h.matmul(PA[:], lhsT=YT[:, c, :], rhs=wo[n][:, c, :], start=(c == 0), stop=(c == 15)),
                      reads=[YT, wo[n]], writes=[PA])
            cs = slice(n * 512, (n + 1) * 512)
            kb.op("dve", lambda h, PA=PA, R=R, cs=cs, gi=gi: h.tensor_tensor(out=R[:, cs], in0=PA[:], in1=rw[:, gi, cs], op=ALU.mult), reads=[PA, rw], writes=[R])
            kb.op("dve", lambda h, X=X, R=R, cs=cs: h.scalar_tensor_tensor(out=R[:, cs], in0=X[:, cs], scalar=float(ALPHA), in1=R[:, cs], op0=ALU.mult, op1=ALU.add),
                  reads=[X, R], writes=[R])
            kb.op("dve", lambda h, R=R, ST=ST, n=n, cs=cs: h.bn_stats(out=ST[:, n, :], in_=R[:, cs]), reads=[R], writes=[ST])
        kb.op("dve", lambda h, ST=ST, MV=MV: h.bn_aggr(out=MV[:], in_=ST[:].rearrange("p a b -> p (a b)")), reads=[ST], writes=[MV])
        kb.op("act", lambda h, MV=MV, RS=RS: h.activation(out=RS[:, 0:1], in_=MV[:, 1:2], func=AF.Sqrt, bias=EPS, scale=1.0), reads=[MV], writes=[RS])
        kb.op("dve", lambda h, RS=RS: h.reciprocal(out=RS[:, 0:1], in_=RS[:, 0:1]), reads=[RS], writes=[RS])
        kb.op("dve", lambda h, RS=RS, MV=MV: h.tensor_scalar(out=RS[:, 1:2], in0=MV[:, 0:1], scalar1=-1.0, scalar2=RS[:, 0:1], op0=ALU.mult, op1=ALU.mult),
              reads=[RS, MV], writes=[RS])
        kb.op("act", lambda h, R=R, RS=RS: h.activation(out=R[:], in_=R[:], func=AF.Identity, scale=RS[:, 0:1], bias=RS[:, 1:2]), reads=[R, RS], writes=[R])
        kb.op("pool", lambda h, R=R, OT=OT: h.tensor_tensor(out=OT[:], in0=R[:], in1=rw[:, 2, :], op=ALU.mult), reads=[R, rw], writes=[OT])
        kb.op("pool", lambda h, OT=OT: h.tensor_tensor(out=OT[:], in0=OT[:], in1=rw[:, 3, :], op=ALU.add), reads=[OT, rw], writes=[OT])
        kb.dma("sp", lambda h, OT=OT, rsl=rsl: h.dma_start(out=x_out[rsl, :], in_=OT[:]), reads=[OT])
    print("s3 instructions:", kb.ninst)
    return kb.finish()


s3 = _NS(); s3.build_s3 = build_s3


def cat_tok(o0, o1, axis):
    def sl(o, a, b):
        idx = [slice(None)] * o.ndim; idx[axis] = slice(a, b); return o[tuple(idx)]
    return np.concatenate([sl(o0, 2048, 2176), sl(o1, 2048, 2176), sl(o0, 0, 2048), sl(o1, 0, 2048)], axis=axis)

_NC = {}
def get_nc(name, fn):
    if name not in _NC:
        _NC[name] = fn()
    return _NC[name]

def run(name, fn, maps):
    t0 = time.time()
    nc = get_nc(name, fn)
    res = run_bass_kernel_spmd(nc, maps, core_ids=list(range(8)))
    print("launch", name, "%.1fs" % (time.time() - t0), flush=True)
    return res.results

def forward(inp):
    f = lambda k: np.ascontiguousarray(np.asarray(inp[k], dtype=np.float32))
    x = f('x'); ctx = f('ctx')
    w_in = f('w_in'); w_out = f('w_out')
    mod = host.mod_assemble(run("mod", s1.build_mod, host.mod_inputs(f('c'), f('c_ctx'), f('w_mod'), f('b_mod'))))
    idn = np.eye(128, dtype=np.float32)
    xcur, ctxcur = x, ctx
    for l in range(2):
        m1 = host.s1_inputs(xcur, ctxcur, mod[l], np.ascontiguousarray(w_in[l]), f('q_norm')[l], f('k_norm')[l])
        o1 = run("s1", s1.build_s1, m1)
        mA = []; mB = []
        mball = host.nbr_tables(f('rpb')[l])
        for core in range(8):
            b = core // 2; g = core % 2
            a0, a1 = o1[2 * b], o1[2 * b + 1]
            mA.append({"a_qT": np.ascontiguousarray(cat_tok(a0['o_qA'], a1['o_qA'], 2)[4 * g:4 * g + 4]),
                       "a_kT": np.ascontiguousarray(cat_tok(a0['o_kA'], a1['o_kA'], 2)[g]),
                       "a_v": np.ascontiguousarray(cat_tok(a0['o_vA'], a1['o_vA'], 0)[:, g * 128:(g + 1) * 128])})
            mB.append({"b_qT": np.ascontiguousarray(cat_tok(a0['o_qB'], a1['o_qB'], 2)[2 * g:2 * g + 2]),
                       "b_kT": np.ascontiguousarray(cat_tok(a0['o_kB'], a1['o_kB'], 2)[2 * g:2 * g + 2]),
                       "b_v": np.ascontiguousarray(cat_tok(a0['o_vB'], a1['o_vB'], 0)[:, g * 256:(g + 1) * 256].reshape(4352, 2, 128).transpose(1, 0, 2)),
                       "b_mb": np.ascontiguousarray(mball[2 * g:2 * g + 2]), "idn": idn})
        oA = run("s2a", s2.build_s2a, mA)
        oB = run("s2b", s2.build_s2b, mB)
        cC = [cat_tok(o1[2 * b]['o_cC'], o1[2 * b + 1]['o_cC'], 2) for b in range(4)]
        abT = [cat_tok(o1[2 * b]['o_ab'], o1[2 * b + 1]['o_ab'], 1) for b in range(4)]
        oC = []
        for half in range(2):
            heads = [(b, hg) for b in range(2 * half, 2 * half + 2) for hg in range(4)]
            mC = host.s2c_inputs(cC, abT, f('conv_w')[l], f('a_log')[l], f('dt_bias')[l], f('o_norm')[l], heads)
            oC += run("s2c", s2c.build_s2c, mC)
        m3 = []
        shift_x, scale_x, gate_x = np.split(mod[l][4], 3)
        for core in range(8):
            b = core // 2; hf = core % 2
            ycat = np.concatenate([oA[2 * b]['y'], oA[2 * b + 1]['y'], oB[2 * b]['y'], oB[2 * b + 1]['y']] +
                                  [oC[b * 4 + hg]['y'] for hg in range(4)], axis=1)
            yc = np.concatenate([ycat[256 + hf * 2048:256 + (hf + 1) * 2048], ycat[hf * 128:(hf + 1) * 128]], axis=0)
            gate = np.split(mod[l][b], 3)[2]
            rows = np.stack([gate, gate_x, f('ln_g')[l], f('ln_b')[l]], 0)[None].repeat(128, 0)
            m3.append({"ycat": np.ascontiguousarray(yc), "sz": o1[core]['o_sz'], "x_in": m1[core]['x_in'],
                       "w_out": np.ascontiguousarray(w_out[l]), "rows": np.ascontiguousarray(rows, dtype=np.float32), "idn": idn})
        o3 = run("s3", s3.build_s3, m3)
        xn = np.zeros_like(x); cn = np.zeros_like(ctx)
        for core in range(8):
            b = core // 2; hf = core % 2
            xo = o3[core]['x_out']
            xn[b, hf * 2048:(hf + 1) * 2048] = xo[:2048]
            cn[b, hf * 128:(hf + 1) * 128] = xo[2048:]
        xcur, ctxcur = xn, cn
    return xcur


def kernel(**inputs):
    out = forward(inputs)
    return np.ascontiguousarray(out.astype(np.float32))
```
